# Optimizing a Trainium2 kernel written in Bass

```python
import math
import jax, jax.numpy as jnp
from jax import lax
import numpy as np

D_MODEL = 1024
BATCH = 2
SEQ = 16384
DEPTH = 2
DEC_BATCH = 4
DEC_SEQ = 8192
PAST_LEN = 128

CHUNK = 128
D_A = 512
G_A = 4
GA_DIM = D_A // G_A
D_B = 512
H_B = 4
HB_DIM = D_B // H_B
D_C = 512
G_C = 4
GC_DIM = D_C // G_C
N_BRANCH = 3
D_FF = 2816
ROPE_BASE = 10000.0
LN_EPS = 1e-5
ALPHA = (2.0 * DEPTH) ** 0.25
BETA = (8.0 * DEPTH) ** -0.25
D_IN = N_BRANCH * D_MODEL + 2 * D_A + 4 * D_B + D_C

kernel_name = "hybrid_gmlp_retention_fnet_encoder"


def layer_norm(x, g, b):
    xf = x.astype(jnp.float32)
    mu = jnp.mean(xf, -1, keepdims=True)
    var = jnp.mean(jnp.square(xf - mu), -1, keepdims=True)
    return ((xf - mu) * lax.rsqrt(var + LN_EPS) * g + b).astype(x.dtype)


def swiglu(x, w_gate, w_up, w_down):
    return (jax.nn.silu(x @ w_gate) * (x @ w_up)) @ w_down


def rotary(x):
    L = x.shape[1]
    half = x.shape[-1] // 2
    inv = ROPE_BASE ** (-jnp.arange(half, dtype=jnp.float32) / half)
    ang = jnp.arange(L, dtype=jnp.float32)[:, None] * inv[None, :]
    cos = jnp.cos(ang)[None, :, None, :]
    sin = jnp.sin(ang)[None, :, None, :]
    x1, x2 = x[..., :half], x[..., half:]
    return jnp.concatenate([x1 * cos - x2 * sin, x1 * sin + x2 * cos], -1)


def retention_dir(q, k, v, log_gamma, include_diag):
    idx = jnp.arange(CHUNK)
    idxf = idx.astype(jnp.float32)
    rel = idx[:, None] - idx[None, :]
    mask = (rel >= 0) if include_diag else (rel > 0)
    relf = jnp.where(mask, rel, 0).astype(jnp.float32)
    dmat = jnp.where(mask[None], jnp.exp(relf[None] * log_gamma[:, None, None]), 0.0)
    scores = jnp.einsum('bnihd,bnjhd->bnhij', q, k) * dmat
    intra = jnp.einsum('bnhij,bnjhe->bnihe', scores, v)
    k_dec = k * jnp.exp((CHUNK - 1 - idxf)[:, None] * log_gamma[None, :])[:, :, None]
    kv = jnp.einsum('bnjhd,bnjhe->nbhde', k_dec, v)
    chunk_decay = jnp.exp(CHUNK * log_gamma)[None, :, None, None]

    def step(state, kv_n):
        return chunk_decay * state + kv_n, state

    _, prev = lax.scan(step, jnp.zeros_like(kv[0]), kv)
    q_dec = q * jnp.exp((idxf + 1.0)[:, None] * log_gamma[None, :])[:, :, None]
    cross = jnp.einsum('bnihd,nbhde->bnihe', q_dec, prev)
    return intra + cross


def retention_branch(q, k, v, g, logit_f, logit_b, gn_g, gn_b):
    bsz, L, _ = q.shape
    n = L // CHUNK
    heads = lambda t: t.astype(jnp.float32).reshape(bsz, L, H_B, HB_DIM)
    qh = rotary(heads(q)) * (HB_DIM ** -0.5)
    kh = rotary(heads(k))
    vh = heads(v)
    chunk = lambda t: t.reshape(bsz, n, CHUNK, H_B, HB_DIM)
    flip = lambda t: chunk(t[:, ::-1])
    lg_f = jax.nn.log_sigmoid(logit_f.astype(jnp.float32))
    lg_b = jax.nn.log_sigmoid(logit_b.astype(jnp.float32))
    fwd = retention_dir(chunk(qh), chunk(kh), chunk(vh), lg_f, True)
    bwd = retention_dir(flip(qh), flip(kh), flip(vh), lg_b, False)
    o = fwd.reshape(bsz, L, H_B, HB_DIM) + bwd.reshape(bsz, L, H_B, HB_DIM)[:, ::-1]
    mu = jnp.mean(o, -1, keepdims=True)
    var = jnp.mean(jnp.square(o - mu), -1, keepdims=True)
    o = ((o - mu) * lax.rsqrt(var + LN_EPS)).reshape(bsz, L, D_B) * gn_g + gn_b
    return (jax.nn.silu(g.astype(jnp.float32)) * o).astype(q.dtype)


def sgu_branch(z, ln_g, ln_b, w_s, b_s):
    bsz, L, _ = z.shape
    z = jax.nn.gelu(z)
    u, v = jnp.split(z, 2, axis=-1)
    v = layer_norm(v, ln_g, ln_b).reshape(bsz, L // CHUNK, CHUNK, G_A, GA_DIM)
    v = jnp.einsum('gij,bnjgc->bnigc', w_s, v) + b_s.T[:, :, None]
    return u * v.reshape(bsz, L, D_A)


def fourier_branch(z):
    bsz, L, _ = z.shape
    zf = z.astype(jnp.float32).reshape(bsz, L, G_C, GC_DIM)
    y = jnp.fft.fft2(zf, axes=(1, 3), norm="ortho").real
    return y.reshape(bsz, L, D_C).astype(z.dtype)


def token_mixer(x, w_in, b_in, sgu_ln_g, sgu_ln_b, sgu_w, sgu_b, logit_f, logit_b,
                gn_g, gn_b, w_br_a, w_br_b, w_br_c, w_o):
    h = x @ w_in + b_in
    sizes = (N_BRANCH * D_MODEL, 2 * D_A, D_B, D_B, D_B, D_B, D_C)
    cuts = [int(c) for c in np.cumsum(sizes)[:-1]]
    gates, za, q, k, v, g, zc = jnp.split(h, cuts, axis=-1)
    ya = sgu_branch(za, sgu_ln_g, sgu_ln_b, sgu_w, sgu_b) @ w_br_a
    yb = retention_branch(q, k, v, g, logit_f, logit_b, gn_g, gn_b) @ w_br_b
    yc = fourier_branch(zc) @ w_br_c
    ga, gb, gc = jnp.split(jax.nn.sigmoid(gates), N_BRANCH, axis=-1)
    return (ga * ya + gb * yb + gc * yc) @ w_o


def trunk(x, params):
    (ffn1_w_gate, ffn1_w_up, ffn1_w_down, ln1_g, ln1_b, w_in, b_in, sgu_ln_g, sgu_ln_b,
     sgu_w, sgu_b, ret_logit_fwd, ret_logit_bwd, ret_gn_g, ret_gn_b, w_br_a, w_br_b, w_br_c,
     w_o, ln2_g, ln2_b, ffn2_w_gate, ffn2_w_up, ffn2_w_down, ln3_g, ln3_b) = params
    for l in range(DEPTH):
        x = layer_norm(ALPHA * x + 0.5 * swiglu(x, ffn1_w_gate[l], ffn1_w_up[l], ffn1_w_down[l]),
                       ln1_g[l], ln1_b[l])
        mix = token_mixer(x, w_in[l], b_in[l], sgu_ln_g[l], sgu_ln_b[l], sgu_w[l], sgu_b[l],
                          ret_logit_fwd[l], ret_logit_bwd[l], ret_gn_g[l], ret_gn_b[l],
                          w_br_a[l], w_br_b[l], w_br_c[l], w_o[l])
        x = layer_norm(ALPHA * x + mix, ln2_g[l], ln2_b[l])
        x = layer_norm(ALPHA * x + 0.5 * swiglu(x, ffn2_w_gate[l], ffn2_w_up[l], ffn2_w_down[l]),
                       ln3_g[l], ln3_b[l])
    return x


def setup_inputs(seed: int = 0) -> dict:
    key = jax.random.key(seed)
    keys = list(jax.random.split(key, 40))

    def nrm(shape, scale):
        return jax.random.normal(keys.pop(), shape, jnp.float32) * scale

    def gain(shape):
        return 1.0 + nrm(shape, 0.02)

    ret_base = jnp.asarray(np.log(2.0 ** (5 + np.arange(H_B)) - 1.0), jnp.float32)[None, :]
    return {
        "x_prompt": nrm((BATCH, SEQ, D_MODEL), 1.0),
        "x_sample": nrm((DEC_BATCH, DEC_SEQ, D_MODEL), 1.0),
        "ffn1_w_gate": nrm((DEPTH, D_MODEL, D_FF), D_MODEL ** -0.5),
        "ffn1_w_up": nrm((DEPTH, D_MODEL, D_FF), D_MODEL ** -0.5),
        "ffn1_w_down": nrm((DEPTH, D_FF, D_MODEL), BETA * D_FF ** -0.5),
        "ln1_g": gain((DEPTH, D_MODEL)),
        "ln1_b": nrm((DEPTH, D_MODEL), 0.02),
        "w_in": nrm((DEPTH, D_MODEL, D_IN), D_MODEL ** -0.5),
        "b_in": nrm((DEPTH, D_IN), 0.02),
        "sgu_ln_g": gain((DEPTH, D_A)),
        "sgu_ln_b": nrm((DEPTH, D_A), 0.02),
        "sgu_w": nrm((DEPTH, G_A, CHUNK, CHUNK), CHUNK ** -0.5),
        "sgu_b": gain((DEPTH, G_A, CHUNK)),
        "ret_logit_fwd": ret_base + nrm((DEPTH, H_B), 0.1),
        "ret_logit_bwd": ret_base + nrm((DEPTH, H_B), 0.1),
        "ret_gn_g": gain((DEPTH, D_B)),
        "ret_gn_b": nrm((DEPTH, D_B), 0.02),
        "w_br_a": nrm((DEPTH, D_A, D_MODEL), D_A ** -0.5),
        "w_br_b": nrm((DEPTH, D_B, D_MODEL), D_B ** -0.5),
        "w_br_c": nrm((DEPTH, D_C, D_MODEL), D_C ** -0.5),
        "w_o": nrm((DEPTH, D_MODEL, D_MODEL), BETA * D_MODEL ** -0.5),
        "ln2_g": gain((DEPTH, D_MODEL)),
        "ln2_b": nrm((DEPTH, D_MODEL), 0.02),
        "ffn2_w_gate": nrm((DEPTH, D_MODEL, D_FF), D_MODEL ** -0.5),
        "ffn2_w_up": nrm((DEPTH, D_MODEL, D_FF), D_MODEL ** -0.5),
        "ffn2_w_down": nrm((DEPTH, D_FF, D_MODEL), BETA * D_FF ** -0.5),
        "ln3_g": gain((DEPTH, D_MODEL)),
        "ln3_b": nrm((DEPTH, D_MODEL), 0.02),
    }


def reference(x_prompt, x_sample, ffn1_w_gate, ffn1_w_up, ffn1_w_down, ln1_g, ln1_b, w_in, b_in,
              sgu_ln_g, sgu_ln_b, sgu_w, sgu_b, ret_logit_fwd, ret_logit_bwd, ret_gn_g, ret_gn_b,
              w_br_a, w_br_b, w_br_c, w_o, ln2_g, ln2_b, ffn2_w_gate, ffn2_w_up, ffn2_w_down,
              ln3_g, ln3_b):
    params = (ffn1_w_gate, ffn1_w_up, ffn1_w_down, ln1_g, ln1_b, w_in, b_in, sgu_ln_g, sgu_ln_b,
              sgu_w, sgu_b, ret_logit_fwd, ret_logit_bwd, ret_gn_g, ret_gn_b, w_br_a, w_br_b,
              w_br_c, w_o, ln2_g, ln2_b, ffn2_w_gate, ffn2_w_up, ffn2_w_down, ln3_g, ln3_b)
    y_prompt = trunk(x_prompt, params)
    y_sample = trunk(x_sample, params)
    return (y_prompt, y_sample)
```

```python
import bisect
import contextlib
import numpy as np
import concourse.bass as bass
import concourse.mybir as mybir
from concourse.bass_utils import run_bass_kernel_spmd

F32 = mybir.dt.float32
BF16 = mybir.dt.bfloat16
AF = mybir.ActivationFunctionType
ALU = mybir.AluOpType

D = 1024
DFF = 2816
TOK = 16384
TS = 512
NT = TOK // TS
NCH = TOK // 128
DEPTH = 2
ALPHA = (2.0 * DEPTH) ** 0.25
LN_EPS = 1e-5
DIN_EXT = 7680
ACTIVE = (0, 1, 4, 5)


class Buf:
    __slots__ = ("name", "writers", "readers", "prev_readers", "semi")

    def __init__(self, name):
        self.name = name
        self.writers = {}
        self.readers = {}
        self.prev_readers = {}
        self.semi = None


class T:
    def __init__(self, h, name):
        self.h = h
        self.b = Buf(name)

    def __getitem__(self, k):
        return self.h[k]


class Sub:
    def __init__(self, parent, idx):
        self.p = parent
        self.idx = idx
        self.b = Buf("%s[%d]" % (parent.b.name, idx))

    def ap(self):
        return self.p.h[:, self.idx, :]


class Eng:
    def __init__(self, name, be, sem):
        self.name = name
        self.be = be
        self.sem = sem
        self.nsig = 0
        self.pos = 0
        self.sig_pos = []
        self.sig_val = []
        self.waited = {}


class Prog:
    def __init__(self, nc, es, n_dma_sems=72):
        self.nc = nc
        self.es = es
        mk = lambda n: es.enter_context(nc.semaphore(n))
        self.pe = Eng("pe", nc.tensor, mk("s_pe"))
        self.act = Eng("act", nc.scalar, mk("s_act"))
        self.dve = Eng("dve", nc.vector, mk("s_dve"))
        self.pool = Eng("pool", nc.gpsimd, mk("s_pool"))
        self.sp = Eng("sp", nc.sync, mk("s_sp"))
        self.engs = [self.pe, self.act, self.dve, self.pool, self.sp]
        self.dsem = [mk("s_d%d" % i) for i in range(n_dma_sems)]
        self.dcnt = [0] * n_dma_sems
        self.n_hw = 44
        self.next_hw = 0
        self.next_sw = 0
        self.bufs = []
        self.rr = 0

    def sb(self, name, shape, dt):
        t = T(self.es.enter_context(self.nc.sbuf_tensor(name, list(shape), dt)), name)
        self.bufs.append(t.b)
        return t

    def ps(self, name, shape, dt=F32):
        t = T(self.es.enter_context(self.nc.psum_tensor(name, list(shape), dt)), name)
        self.bufs.append(t.b)
        return t

    def subs(self, t, n):
        out = [Sub(t, i) for i in range(n)]
        for u in out:
            self.bufs.append(u.b)
        return out

    def dram(self, name, shape, dt, kind="Internal"):
        h = self.nc.dram_tensor(name, list(shape), dt, kind=kind)
        t = T(h.ap(), name)
        self.bufs.append(t.b)
        return t

    def _deps(self, reads, writes):
        deps = []
        for t in reads:
            deps.extend(t.b.writers.values())
        for t in writes:
            b = t.b
            if b.readers:
                b.prev_readers = b.readers
                b.readers = {}
                b.writers = {}
            deps.extend(b.prev_readers.values())
        return deps

    def _wait(self, E, deps):
        need = {}
        for d in deps:
            if d[0] == "dma":
                _, si, val = d
                key = ("d", si)
                sem = self.dsem[si]
            else:
                _, X, pos = d
                if X is E and E is self.pe:
                    continue
                i = bisect.bisect_left(X.sig_pos, pos)
                if i >= len(X.sig_pos):
                    raise RuntimeError("dependency on non-signalled op of %s" % X.name)
                val = X.sig_val[i]
                key = ("e", X.name)
                sem = X.sem
            if need.get(key, (None, 0))[1] < val:
                need[key] = (sem, val)
        for key, (sem, val) in need.items():
            if E.waited.get(key, 0) < val:
                E.be.wait_ge(sem, val)
                E.waited[key] = val

    def op(self, E, fn, reads=(), writes=(), sig=True):
        deps = self._deps(reads, writes)
        self._wait(E, deps)
        ins = fn()
        rec = ("e", E, E.pos)
        if sig:
            E.nsig += 1
            ins.then_inc(E.sem, 1)
            E.sig_pos.append(E.pos)
            E.sig_val.append(E.nsig)
        E.pos += 1
        for t in reads:
            t.b.readers[E.name] = rec
        for t in writes:
            t.b.writers[E.name] = rec
        return ins

    def dma(self, Q, out, in_, reads, writes, semt):
        deps = self._deps(reads, writes)
        self._wait(Q, deps)
        b = semt.b
        if b.semi is None:
            b.semi = {}
        if Q.name not in b.semi:
            if Q is self.sp:
                b.semi[Q.name] = self.next_hw
                self.next_hw += 1
                if self.next_hw > self.n_hw:
                    raise RuntimeError("out of hw dma semaphores")
            else:
                b.semi[Q.name] = self.n_hw + self.next_sw
                self.next_sw += 1
                if self.n_hw + self.next_sw > len(self.dsem):
                    raise RuntimeError("out of sw dma semaphores")
        si = b.semi[Q.name]
        self.dcnt[si] += 16
        Q.be.dma_start(out=out, in_=in_).then_inc(self.dsem[si], 16)
        rec = ("dma", si, self.dcnt[si])
        key = "dma%d" % si
        for t in reads:
            t.b.readers[key] = rec
        for t in writes:
            t.b.writers[key] = rec

    def barrier(self):
        for W in self.engs:
            for X in self.engs:
                if X is W or X.pos == 0:
                    continue
                if X.sig_pos[-1] != X.pos - 1:
                    raise RuntimeError("last op of %s not signalled" % X.name)
                key = ("e", X.name)
                if W.waited.get(key, 0) < X.nsig:
                    W.be.wait_ge(X.sem, X.nsig)
                    W.waited[key] = X.nsig
            for si in range(len(self.dsem)):
                key = ("d", si)
                if self.dcnt[si] and W.waited.get(key, 0) < self.dcnt[si]:
                    W.be.wait_ge(self.dsem[si], self.dcnt[si])
                    W.waited[key] = self.dcnt[si]
        for b in self.bufs:
            b.writers = {}
            b.readers = {}
            b.prev_readers = {}

    def new_phase_sems(self):
        self.next_hw = 0
        self.next_sw = 0
        for b in self.bufs:
            b.semi = None

    def mm(self, out, lhsT, rhs, start, stop, reads, writes, sig=None):
        if sig is None:
            sig = stop
        return self.op(self.pe, lambda: self.nc.tensor.matmul(out, lhsT, rhs, start=start, stop=stop),
                       reads, writes, sig)

    def tr(self, out, in_, ident, reads, writes, sig=True):
        return self.op(self.pe, lambda: self.nc.tensor.transpose(out, in_, ident), reads, writes, sig)

    def actf(self, out, in_, func, reads, writes, bias=None, scale=None):
        kw = {}
        if bias is not None:
            kw["bias"] = bias
        if scale is not None:
            kw["scale"] = scale
        return self.op(self.act, lambda: self.nc.scalar.activation(out, in_, func, **kw), reads, writes)

    def ve(self, name):
        return {"dve": (self.dve, self.nc.vector), "pool": (self.pool, self.nc.gpsimd)}[name]

    def tt(self, eng, out, in0, in1, op, reads, writes):
        E, be = self.ve(eng)
        return self.op(E, lambda: be.tensor_tensor(out, in0, in1, op), reads, writes)

    def stt(self, eng, out, in0, scalar, in1, op0, op1, reads, writes):
        E, be = self.ve(eng)
        return self.op(E, lambda: be.scalar_tensor_tensor(out, in0, scalar, in1, op0, op1), reads, writes)

    def ts(self, eng, out, in0, s1, s2, op0, op1, reads, writes):
        E, be = self.ve(eng)
        if s2 is None:
            return self.op(E, lambda: be.tensor_scalar(out, in0, s1, None, op0), reads, writes)
        return self.op(E, lambda: be.tensor_scalar(out, in0, s1, s2, op0, op1), reads, writes)

    def cp(self, eng, out, in_, reads, writes):
        if eng == "act":
            return self.op(self.act, lambda: self.nc.scalar.copy(out, in_), reads, writes)
        E, be = self.ve(eng)
        return self.op(E, lambda: be.tensor_copy(out, in_), reads, writes)

    def cp_rr(self, out, in_, reads, writes, engs=("act", "dve", "pool")):
        e = engs[self.rr % len(engs)]
        self.rr += 1
        return self.cp(e, out, in_, reads, writes)


WSPECS = [
    ("ffn1_w_gate", D, DFF, 256), ("ffn1_w_up", D, DFF, 256), ("ffn1_w_down", DFF, D, 128),
    ("w_in", D, DIN_EXT, 256),
    ("w_br_a", 512, D, 512), ("w_br_b", 512, D, 512), ("w_br_c", 512, D, 512),
    ("w_o", D, D, 256),
    ("ffn2_w_gate", D, DFF, 256), ("ffn2_w_up", D, DFF, 256), ("ffn2_w_down", DFF, D, 128),
]
WBUF_ELEMS = 22 * 128
NWBUF = 4


class Ctx:
    pass


NCPK = 1668
NLP = 3080
GCOLS = 65536


def build_program(stop_after=None, dbg=()):
    nc = bass.Bass("TRN2", target_bir_lowering=False)
    es = contextlib.ExitStack()
    P = Prog(nc, es)
    C = Ctx()
    sa = stop_after or {}
    NTL = sa.get("ntiles", NT)
    NLAY = sa.get("layers", DEPTH)
    phases = sa.get("phases", 3)

    def dkind(name):
        return "ExternalOutput" if name in dbg else "Internal"

    xT = P.dram("xT", [D, TOK], F32, "ExternalInput")
    yT = P.dram("yT", [D, TOK], F32, "ExternalOutput")
    win = {}
    for l in range(DEPTH):
        for (nm, K, N, sw) in WSPECS:
            win[(l, nm)] = P.dram("%s_%d" % (nm, l), [K, N], F32, "ExternalInput")
    vecp = [P.dram("vecp_%d" % l, [128, 116], F32, "ExternalInput") for l in range(DEPTH)]
    lpd = [P.dram("lp_%d" % l, [128, NLP], F32, "ExternalInput") for l in range(DEPTH)]
    cpkd = P.dram("cpk", [128, NCPK], F32, "ExternalInput")
    cossin = P.dram("cossin", [128, 2, TOK], F32, "ExternalInput")
    gtab = P.dram("gtab", [128, GCOLS], F32, "ExternalInput")
    gsc = P.dram("gsc", [128, GCOLS], BF16)
    wsc = {}
    for l in range(DEPTH):
        for (nm, K, N, sw) in WSPECS:
            wsc[(l, nm)] = P.dram("wb_%s_%d" % (nm, l), [N // sw, 128, K // 128, sw], BF16)
    X1 = P.dram("X1", [D, TOK], F32, dkind("X1"))
    XA = P.dram("XA", [D, TOK], F32, dkind("XA"))
    GT = P.dram("GT", [3072, TOK], BF16, dkind("GT"))
    SGT = P.dram("SGT", [512, TOK], BF16, dkind("SGT"))
    QT = P.dram("QT", [512, TOK], BF16, dkind("QT"))
    QFT = P.dram("QFT", [512, TOK], BF16, dkind("QFT"))
    QBT = P.dram("QBT", [512, TOK], BF16, dkind("QBT"))
    KT = P.dram("KT", [512, TOK], BF16, dkind("KT"))
    KDF = P.dram("KDF", [TOK, 512], BF16, dkind("KDF"))
    KDB = P.dram("KDB", [TOK, 512], BF16, dkind("KDB"))
    VR = P.dram("VR", [TOK, 512], BF16, dkind("VR"))
    GST = P.dram("GST", [512, TOK], BF16, dkind("GST"))
    ZC = P.dram("ZC", [4, TOK, 256], BF16, dkind("ZC"))
    SBD = P.dram("SBD", [NCH, 128, 512], BF16, dkind("SBD"))
    RT = P.dram("RT", [512, TOK], BF16, dkind("RT"))
    FT = P.dram("FT", [512, TOK], BF16, dkind("FT"))

    PS = [P.ps("ps%d" % i, [128, 512], F32) for i in range(8)]
    ones_bf = P.sb("ones_bf", [128, 128], BF16)
    ones128 = P.sb("ones128", [128, 128], BF16)
    epst = P.sb("epst", [128, 2], F32)
    vp = [P.sb("vp%d" % l, [128, 116], F32) for l in range(DEPTH)]
    cpk = P.sb("cpk_sb", [128, NCPK], F32)
    cbf = P.sb("cbf", [128, 896], BF16)
    lp = P.sb("lp_sb", [128, NLP], F32)
    wsT = P.sb("wsT", [128, 512], BF16)
    lg = P.sb("lg", [128, 8], F32)
    maskT = P.sb("maskT", [128, 512], F32)
    DQ = P.sb("DQ", [128, 2, 4, 128], F32)
    KD = P.sb("KD", [128, 2, 4], F32)
    CD = P.sb("CD", [128, 2, 4], F32)
    mtmp = P.sb("mtmp", [128, 2, 128], F32)
    C.wrr = 0
    ident = cbf[:, 0:128]
    FCb = cbf[:, 128:384]
    W1a = cbf[:, 384:640]
    W1b = cbf[:, 640:896]
    O_EF, O_IF, O_EB, O_IB, O_IP1, O_IP2, O_JC1, O_JC2, O_FLAG = 896, 1024, 1152, 1280, 1408, 1536, 1664, 1665, 1666
    L_WST, L_BVS, L_LNG, L_LNB, L_BVR, L_SB, L_LF, L_LB = 0, 512, 1024, 1536, 2048, 2560, 3072, 3076

    P.op(P.pool, lambda: nc.gpsimd.memset(ones_bf[:], 1.0 / D), [], [ones_bf])
    P.op(P.pool, lambda: nc.gpsimd.memset(ones128[:], 1.0 / 128), [], [ones128])
    P.op(P.pool, lambda: nc.gpsimd.memset(epst[:, 0:1], LN_EPS / (ALPHA * ALPHA)), [], [epst])
    P.op(P.pool, lambda: nc.gpsimd.memset(epst[:, 1:2], LN_EPS), [], [epst])
    for l in range(DEPTH):
        P.dma(P.sp, vp[l][:], vecp[l][:, :], [vecp[l]], [vp[l]], vp[l])
    P.dma(P.sp, cpk[:], cpkd[:, :], [cpkd], [cpk], cpk)
    P.cp("dve", cbf[:, :], cpk[:, 0:896], [cpk], [cbf])

    CW = 2048
    C.pending = None
    with contextlib.ExitStack() as es0:
        P.es = es0
        cin = [P.sb("cin%d" % i, [128, CW], F32) for i in range(3)]
        cout = [P.sb("cout%d" % i, [128, CW], BF16) for i in range(3)]
        k = 0
        def make_jobs(layers, with_g):
            out = []
            srcs = [(win[(l_, nm)], wsc[(l_, nm)], K, N, sw) for l_ in layers for (nm, K, N, sw) in WSPECS]
            if with_g:
                srcs.append((gtab, gsc, 128, GCOLS, None))
            for (src, dst, K, N, sw) in srcs:
                for kc in range(K // 128):
                    for c0 in range(0, N, CW):
                        out.append((src, dst, sw, kc, c0, min(CW, N - c0)))
            return out

        def cast_job(job, ci, co, eng):
            src, dst, sw, kc, c0, cw = job
            P.dma(P.sp, ci[:, 0:cw], src[kc * 128:(kc + 1) * 128, c0:c0 + cw], [src], [ci], ci)
            P.cp(eng, co[:, 0:cw], ci[:, 0:cw], [ci], [co])
            if sw is None:
                P.dma(P.pool, dst[:, c0:c0 + cw], co[:, 0:cw], [co], [dst], co)
            else:
                assert cw % sw == 0 and c0 % sw == 0
                ns = cw // sw
                s0 = c0 // sw
                P.dma(P.pool, dst[s0:s0 + ns, :, kc, :].rearrange("s p w -> p s w"),
                      co[:, 0:cw].rearrange("p (s w) -> p s w", w=sw), [co], [dst], co)

        C.make_jobs, C.cast_job = make_jobs, cast_job
        for job in make_jobs([0], False):
            cast_job(job, cin[k % 3], cout[k % 3], ("act", "dve", "pool")[k % 3])
            k += 1
        P.barrier()
    P.es = es
    P.new_phase_sems()

    def load_w(wbuf, l, nm, s):
        K, N, sw = [(a, b, c) for (n_, a, b, c) in WSPECS if n_ == nm][0]
        kc = K // 128
        wb = wbuf[C.wrr % len(wbuf)]
        C.wrr += 1
        src = wsc[(l, nm)]
        view = wb[:, 0:kc * sw].rearrange("p (k w) -> p k w", w=sw)
        P.dma(P.sp, view, src[s], [src], [wb], wb)
        return wb, view

    def layer_setup(l):
        P.dma(P.sp, lp[:], lpd[l][:, :], [lpd[l]], [lp], lp)
        P.cp("dve", wsT[:, :], lp[:, L_WST:L_WST + 512], [lp], [wsT])
        P.actf(lg[:, :], lp[:, L_LF:L_LF + 8], AF.Sigmoid, [lp], [lg])
        P.actf(lg[:, :], lg[:, :], AF.Ln, [lg], [lg])
        for h in range(4):
            P.actf(mtmp[:, 0, :], cpk[:, O_EF:O_EF + 128], AF.Exp, [cpk, lg], [mtmp], scale=lg[:, h:h + 1])
            P.tt("dve", mtmp[:, 0, :], mtmp[:, 0, :], cpk[:, O_IF:O_IF + 128], ALU.mult, [mtmp, cpk], [mtmp])
            P.actf(mtmp[:, 1, :], cpk[:, O_EB:O_EB + 128], AF.Exp, [cpk, lg], [mtmp], scale=lg[:, 4 + h:5 + h])
            P.tt("dve", mtmp[:, 1, :], mtmp[:, 1, :], cpk[:, O_IB:O_IB + 128], ALU.mult, [mtmp, cpk], [mtmp])
            P.tt("dve", maskT[:, h * 128:(h + 1) * 128], mtmp[:, 0, :], mtmp[:, 1, :], ALU.add, [mtmp], [maskT])
            for dr, off in ((0, O_IP1), (1, O_IP2)):
                P.actf(DQ[:, dr, h, :], cpk[:, off:off + 128], AF.Exp, [cpk, lg], [DQ],
                       scale=lg[:, 4 * dr + h:4 * dr + h + 1])
        P.ts("dve", DQ[:, :, :, :], DQ[:, :, :, :], 128.0 ** -0.5, None, ALU.mult, None, [DQ], [DQ])
        P.actf(KD[:, 0, :], lg[:, 0:4], AF.Exp, [lg, cpk], [KD], scale=cpk[:, O_JC1:O_JC1 + 1])
        P.actf(KD[:, 1, :], lg[:, 4:8], AF.Exp, [lg, cpk], [KD], scale=cpk[:, O_JC2:O_JC2 + 1])
        P.actf(CD[:, 0, :], lg[:, 0:4], AF.Exp, [lg], [CD], scale=128.0)
        P.actf(CD[:, 1, :], lg[:, 4:8], AF.Exp, [lg], [CD], scale=128.0)

    def ln_pre(c, zs, lntmp):
        zrot = lntmp[4]
        zb_, zq_ = zrot[(2 * c) % 4], zrot[(2 * c + 1) % 4]
        P.actf(zb_[:, :], zs[c].ap(), AF.Copy, [zs[c]], [zb_])
        P.actf(zq_[:, :], zs[c].ap(), AF.Square, [zs[c]], [zq_])
        return zb_, zq_

    def ln_stats_mm(c, zb_, zq_):
        pm, pq = PS[6], PS[7]
        P.mm(pm[:, :], ones_bf[:, :], zb_[:, :], c == 0, c == 7, [ones_bf, zb_], [pm])
        P.mm(pq[:, :], ones_bf[:, :], zq_[:, :], c == 0, c == 7, [ones_bf, zq_], [pq])

    def ln_post(zs, x1bs, lntmp, vpl, g_col, b_col):
        m2, rstd, nmr = lntmp[0], lntmp[1], lntmp[2]
        pm, pq = PS[6], PS[7]
        P.actf(m2[:, :], pm[:, :], AF.Square, [pm], [m2])
        P.tt("dve", rstd[:, :], pq[:, :], m2[:, :], ALU.subtract, [pq, m2], [rstd])
        P.actf(rstd[:, :], rstd[:, :], AF.Ln, [rstd, epst], [rstd], bias=epst[:, 0:1])
        P.actf(rstd[:, :], rstd[:, :], AF.Exp, [rstd], [rstd], scale=-0.5)
        P.stt("dve", nmr[:, :], pm[:, :], -1.0, rstd[:, :], ALU.mult, ALU.mult, [pm, rstd], [nmr])
        for c in range(8):
            P.tt("dve", zs[c].ap(), zs[c].ap(), rstd[:, :], ALU.mult, [zs[c], rstd], [zs[c]])
            P.tt("pool", zs[c].ap(), zs[c].ap(), nmr[:, :], ALU.add, [zs[c], nmr], [zs[c]])
            P.actf(x1bs[c].ap(), zs[c].ap(), AF.Identity, [zs[c], vpl], [x1bs[c]],
                   bias=vpl[:, b_col + c:b_col + c + 1], scale=vpl[:, g_col + c:g_col + c + 1])
            P.actf(zs[c].ap(), zs[c].ap(), AF.Identity, [zs[c], vpl], [zs[c]],
                   bias=vpl[:, b_col + c:b_col + c + 1], scale=vpl[:, g_col + c:g_col + c + 1])

    def ffn_ln(l, which, W, xts, xbs, hb, zs, x1bs, lntmp, g_col, b_col, resid_scale):
        pre = "ffn%d_" % which
        sgt = lntmp[3]
        xb_t = xbs[0].p
        for s in range(DFF // 256):
            wg, wgv = load_w(W, l, pre + "w_gate", s)
            wu, wuv = load_w(W, l, pre + "w_up", s)
            for j in range(2):
                m = 2 * s + j
                pg, pu = PS[(m % 2) * 2], PS[(m % 2) * 2 + 1]
                for kc in range(8):
                    P.mm(pg[:, :], wgv[:, kc, j * 128:(j + 1) * 128], xb_t[:, kc, :], kc == 0, kc == 7, [wg, xbs[kc]], [pg])
                for kc in range(8):
                    P.mm(pu[:, :], wuv[:, kc, j * 128:(j + 1) * 128], xb_t[:, kc, :], kc == 0, kc == 7, [wu, xbs[kc]], [pu])
                sg = sgt[m % 2]
                P.actf(sg[:, :], pg[:, :], AF.Silu, [pg], [sg])
                P.tt("dve", hb[:, m, :], sg[:, :], pu[:, :], ALU.mult, [sg, pu], [hb])
        prev = None
        for c in range(8):
            wd, wdv = load_w(W, l, pre + "w_down", c)
            pd = PS[4 + (c % 2)]
            for kc in range(22):
                P.mm(pd[:, :], wdv[:, kc, :], hb[:, kc, :], kc == 0, kc == 21, [wd, hb], [pd])
            if prev is not None:
                ln_stats_mm(c - 1, *prev)
            P.stt("dve", zs[c].ap(), pd[:, :], resid_scale, xts[c].ap(), ALU.mult, ALU.add, [pd, xts[c]], [zs[c]])
            prev = ln_pre(c, zs, lntmp)
        ln_stats_mm(7, *prev)
        ln_post(zs, x1bs, lntmp, vp[l], g_col, b_col)

    def common_tiles(sfx, nx=2):
        W = [P.sb("wbuf%d%s" % (i, sfx), [128, WBUF_ELEMS], BF16) for i in range(NWBUF)]
        xt = [P.sb("xt%d%s" % (i, sfx), [128, 8, 512], F32) for i in range(nx)]
        xb = P.sb("xb" + sfx, [128, 8, 512], BF16)
        hb = P.sb("hb" + sfx, [128, 22, 512], BF16)
        z = P.sb("z" + sfx, [128, 8, 512], F32)
        x1b = P.sb("x1b" + sfx, [128, 8, 512], BF16)
        lntmp = (P.sb("m2" + sfx, [128, 512], F32), P.sb("rstd" + sfx, [128, 512], F32),
                 P.sb("nmr" + sfx, [128, 512], F32),
                 [P.sb("sg0" + sfx, [128, 512], F32), P.sb("sg1" + sfx, [128, 512], F32)],
                 [P.sb("zr%d%s" % (i, sfx), [128, 512], BF16) for i in range(4)])
        return W, xt, xb, hb, z, x1b, lntmp

    BOFF = 48

    def phase1(l):
        sfx = "_p1_%d" % l
        W, xt, xb, hb, z, x1b, lntmp = common_tiles(sfx)
        xts = [P.subs(x_, 8) for x_ in xt]
        xbs, zs, x1bs = P.subs(xb, 8), P.subs(z, 8), P.subs(x1b, 8)
        U = P.sb("U" + sfx, [128, 4, 512], F32)
        stg = [P.sb("stg%d%s" % (i, sfx), [128, 4, 512], BF16) for i in range(3)]
        qded = [P.sb("qded%d%s" % (i, sfx), [128, 4, 512], BF16) for i in range(3)]
        C.srr = 0

        def stage():
            t_ = stg[C.srr % len(stg)]
            C.srr += 1
            return t_
        cs = [P.sb("cs%d%s" % (i, sfx), [128, 2, 512], F32) for i in range(2)]
        r1 = P.sb("r1" + sfx, [128, 512], F32)
        r2 = P.sb("r2" + sfx, [128, 512], F32)
        vt1 = [P.sb("vt1%d%s" % (i, sfx), [128, 512], F32) for i in range(1)] * 2
        vt2 = [P.sb("vt2%d%s" % (i, sfx), [128, 512], F32) for i in range(1)] * 2
        vsg = [P.sb("vsg%d%s" % (i, sfx), [128, 512], BF16) for i in range(4)]
        st6 = P.sb("st6" + sfx, [128, 6], F32)
        mv = P.sb("mv" + sfx, [128, 4], F32)
        src = xT if l == 0 else XA
        vpl = vp[l]

        def fm(ap_dram, tsl):
            return ap_dram[:, tsl].rearrange("(c p) t -> p c t", p=128)

        def load_x(t):
            tsl_ = slice(t * TS, (t + 1) * TS)
            P.dma(P.sp, xt[t % 2][:, :, :], fm(src, tsl_), [src], xts[t % 2], xt[t % 2])
            P.dma(P.sp, cs[t % 2][:, :, :], cossin[:, :, tsl_], [cossin], [cs[t % 2]], cs[t % 2])

        def cast_x(t):
            for c in range(8):
                P.cp("act" if c % 2 else "dve", xbs[c].ap(), xts[t % 2][c].ap(), [xts[t % 2][c]], [xbs[c]])

        load_x(0)
        cast_x(0)
        for t in range(NTL):
            tsl = slice(t * TS, (t + 1) * TS)
            cst = cs[t % 2]
            if t + 1 < NTL:
                load_x(t + 1)
            ffn_ln(l, 1, W, xts[t % 2], xbs, hb, zs, x1bs, lntmp, 0, 8, 0.5 / ALPHA)
            P.dma(P.pool, fm(X1, tsl), z[:, :, :], zs, [X1], z)
            if t + 1 < NTL:
                cast_x(t + 1)

            def fmm(bank, wv, j, wb):
                for kc in range(8):
                    P.mm(bank[:, :], wv[:, kc, j * 128:(j + 1) * 128], x1b[:, kc, :], kc == 0, kc == 7, [wb, x1bs[kc]], [bank])

            def tmm(wb, wv, half):
                for tc in range(4):
                    bank = PS[4 + tc]
                    for kc in range(8):
                        P.mm(bank[:, half * 256:(half + 1) * 256], x1b[:, kc, tc * 128:(tc + 1) * 128], wv[:, kc, :],
                             kc == 0, kc == 7, [wb, x1bs[kc]], [bank])

            def G(i):
                gs = stage()
                for s_ in (2 * i, 2 * i + 1):
                    wb, wv = load_w(W, l, "w_in", s_)
                    for j in range(2):
                        ci = 2 * s_ + j
                        bank = PS[ci % 4]
                        fmm(bank, wv, j, wb)
                        P.actf(gs[:, ci % 4, :], bank[:, :], AF.Sigmoid, [bank, vpl], [gs], bias=vpl[:, BOFF + ci:BOFF + ci + 1])
                P.dma(P.pool, GT[i * 512:(i + 1) * 512, tsl].rearrange("(c p) t -> p c t", p=128), gs[:, :, :], [gs], [GT], gs)

            def u_slabs():
                for s_ in (12, 13):
                    wb, wv = load_w(W, l, "w_in", s_)
                    for j in range(2):
                        ci = 2 * s_ + j
                        bank = PS[ci % 4]
                        fmm(bank, wv, j, wb)
                        P.actf(U[:, ci - 24, :], bank[:, :], AF.Gelu, [bank, vpl], [U], bias=vpl[:, BOFF + ci:BOFF + ci + 1])

            def vs_mm():
                for half, s_ in enumerate((14, 15)):
                    wb, wv = load_w(W, l, "w_in", s_)
                    tmm(wb, wv, half)

            def vs_chain(tc):
                    bank = PS[4 + tc]
                    a1, a2 = vt1[tc % 2], vt2[tc % 2]
                    P.tt("dve", a1[:, :], bank[:, :], lp[:, L_BVS:L_BVS + 512], ALU.add, [bank, lp], [a1])
                    P.actf(a2[:, :], a1[:, :], AF.Gelu, [a1], [a2])
                    P.op(P.dve, lambda: nc.vector.bn_stats(st6[:, :], a2[:, :]), [a2], [st6])
                    P.op(P.dve, lambda: nc.vector.bn_aggr(mv[:, 0:2], st6[:, :]), [st6], [mv])
                    P.actf(mv[:, 2:3], mv[:, 1:2], AF.Ln, [mv, epst], [mv], bias=epst[:, 1:2])
                    P.actf(mv[:, 3:4], mv[:, 2:3], AF.Exp, [mv], [mv], scale=-0.5)
                    P.ts("dve", a1[:, :], a2[:, :], mv[:, 0:1], mv[:, 3:4], ALU.subtract, ALU.mult, [a2, mv], [a1])
                    P.tt("pool", a1[:, :], a1[:, :], lp[:, L_LNG:L_LNG + 512], ALU.mult, [a1, lp], [a1])
                    P.tt("pool", vsg[tc][:, :], a1[:, :], lp[:, L_LNB:L_LNB + 512], ALU.add, [a1, lp], [vsg[tc]])

            def sgu_block():
                for g in range(4):
                    bank = PS[4 + g]
                    for tc in range(4):
                        P.mm(bank[:, tc * 128:(tc + 1) * 128], vsg[tc][:, g * 128:(g + 1) * 128], wsT[:, g * 128:(g + 1) * 128],
                             True, True, [vsg[tc], wsT], [bank], sig=(tc == 3))
                sgs = stage()
                for g in range(4):
                    bank = PS[4 + g]
                    bsb = lp[:, L_SB + g * 128:L_SB + (g + 1) * 128].unsqueeze(1).to_broadcast([128, 4, 128])
                    P.tt("dve", r1[:, :].rearrange("p (a b) -> p a b", b=128), bank[:, :].rearrange("p (a b) -> p a b", b=128),
                         bsb, ALU.add, [bank, lp], [r1])
                    P.tt("pool", sgs[:, g, :], r1[:, :], U[:, g, :], ALU.mult, [r1, U], [sgs])
                P.dma(P.pool, fm(SGT, tsl), sgs[:, :, :], [sgs], [SGT], sgs)


            def qk_head(qk, h, outs):
                s_ = 16 + qk * 4 + h
                wb, wv = load_w(W, l, "w_in", s_)
                b0, b1 = PS[(h % 2) * 2], PS[(h % 2) * 2 + 1]
                fmm(b0, wv, 0, wb)
                fmm(b1, wv, 1, wb)
                ci = 2 * s_
                P.stt("dve", r1[:, :], b0[:, :], vpl[:, BOFF + ci:BOFF + ci + 1], cst[:, 0, :], ALU.add, ALU.mult,
                      [b0, vpl, cst], [r1])
                P.stt("dve", r2[:, :], b1[:, :], vpl[:, BOFF + ci + 1:BOFF + ci + 2], cst[:, 1, :], ALU.add, ALU.mult,
                      [b1, vpl, cst], [r2])
                P.tt("pool", r1[:, :], r1[:, :], r2[:, :], ALU.add, [r1, r2], [r1])
                if qk == 0:
                    qs, qfs, qbs = outs
                    P.op(P.act, lambda: nc.scalar.mul(qs[:, h, :], r1[:, :], 128.0 ** -0.5), [r1], [qs])
                    r1v = r1[:, :].rearrange("p (a b) -> p a b", b=128)
                    P.tt("pool", qfs[:, h, :].rearrange("p (a b) -> p a b", b=128), r1v,
                         DQ[:, 0, h:h + 1, :].to_broadcast([128, 4, 128]), ALU.mult, [r1, DQ], [qfs])
                    P.tt("dve", qbs[:, h, :].rearrange("p (a b) -> p a b", b=128), r1v,
                         DQ[:, 1, h:h + 1, :].to_broadcast([128, 4, 128]), ALU.mult, [r1, DQ], [qbs])
                else:
                    P.cp("act", outs[:, h, :], r1[:, :], [r1], [outs])

            def tokrows(ap_dram):
                return ap_dram[tsl, :].rearrange("(c p) e -> p c e", p=128)

            def k_finish(kbf):
                kfs, kbs = stage(), stage()
                for tc in range(4):
                    bank = PS[4 + tc]
                    bv = bank[:, :].bitcast(BF16)
                    for h in range(4):
                        P.tr(bv[:, h * 128:(h + 1) * 128], kbf[:, h, tc * 128:(tc + 1) * 128], ident, [kbf, cbf], [bank], sig=(h == 3))
                    bv3 = bv[:, 0:512].rearrange("p (a b) -> p a b", b=128)
                    P.tt("dve", kfs[:, tc, :].rearrange("p (a b) -> p a b", b=128), bv3,
                         KD[:, 0, :].unsqueeze(2).to_broadcast([128, 4, 128]), ALU.mult, [bank, KD], [kfs])
                    P.tt("dve", kbs[:, tc, :].rearrange("p (a b) -> p a b", b=128), bv3,
                         KD[:, 1, :].unsqueeze(2).to_broadcast([128, 4, 128]), ALU.mult, [bank, KD], [kbs])
                P.dma(P.pool, tokrows(KDF), kfs[:, :, :], [kfs], [KDF], kfs)
                P.dma(P.pool, tokrows(KDB), kbs[:, :, :], [kbs], [KDB], kbs)

            def vret():
                for half, s_ in enumerate((24, 25)):
                    wb, wv = load_w(W, l, "w_in", s_)
                    tmm(wb, wv, half)
                vrs = stage()
                for tc in range(4):
                    P.tt("dve", vrs[:, tc, :], PS[4 + tc][:, :], lp[:, L_BVR:L_BVR + 512], ALU.add, [PS[4 + tc], lp], [vrs])
                P.dma(P.pool, tokrows(VR), vrs[:, :, :], [vrs], [VR], vrs)

            def g_slab(s_, gss):
                wb, wv = load_w(W, l, "w_in", s_)
                for j in range(2):
                    ci = 2 * s_ + j
                    bank = PS[ci % 4]
                    fmm(bank, wv, j, wb)
                    P.actf(gss[:, ci - 52, :], bank[:, :], AF.Silu, [bank, vpl], [gss], bias=vpl[:, BOFF + ci:BOFF + ci + 1])

            def zc_slabs(zcb):
                for s_ in (28, 29):
                    wb, wv = load_w(W, l, "w_in", s_)
                    for j in range(2):
                        ci = 2 * s_ + j
                        bank = PS[ci % 4]
                        fmm(bank, wv, j, wb)
                        P.actf(zcb[:, ci - 56, :], bank[:, :], AF.Identity, [bank, vpl], [zcb], bias=vpl[:, BOFF + ci:BOFF + ci + 1])

            def zc_dft(zcb):
                for tc in range(4):
                    zst = stage()
                    for g2 in range(2):
                        bank = PS[4 + (tc % 2) * 2 + g2]
                        for gg in range(2):
                            g = 2 * g2 + gg
                            P.mm(bank[:, gg * 256:(gg + 1) * 256], zcb[:, g, tc * 128:(tc + 1) * 128], FCb, True, True,
                                 [zcb, cbf], [bank], sig=(gg == 1))
                        P.cp("act" if g2 == 0 else "dve", zst[:, g2, :], bank[:, :], [bank], [zst])
                    P.dma(P.pool, ZC[:, t * TS + tc * 128:t * TS + (tc + 1) * 128, :].rearrange("g p w -> p g w"),
                          zst[:, 0:2, :].rearrange("p a (b w) -> p (a b) w", w=256), [zst], [ZC], zst)

            vs_mm()
            qouts = tuple(qded)
            for h in range(4):
                G(h)
                vs_chain(h)
                qk_head(0, h, qouts)
            P.dma(P.pool, fm(QT, tsl), qouts[0][:, :, :], [qouts[0]], [QT], qouts[0])
            P.dma(P.pool, fm(QFT, tsl), qouts[1][:, :, :], [qouts[1]], [QFT], qouts[1])
            P.dma(P.pool, fm(QBT, tsl), qouts[2][:, :, :], [qouts[2]], [QBT], qouts[2])
            u_slabs()
            sgu_block()
            kbf = qded[0]
            G(4)
            qk_head(1, 0, kbf)
            G(5)
            qk_head(1, 1, kbf)
            gss = stage()
            g_slab(26, gss)
            qk_head(1, 2, kbf)
            g_slab(27, gss)
            P.dma(P.pool, fm(GST, tsl), gss[:, :, :], [gss], [GST], gss)
            qk_head(1, 3, kbf)
            P.dma(P.pool, fm(KT, tsl), kbf[:, :, :], [kbf], [KT], kbf)
            vret()
            zcb = qded[1]
            zc_slabs(zcb)
            k_finish(kbf)
            zc_dft(zcb)

    def phase2_ret(l):
        sfx = "_p2r_%d" % l
        vpl = vp[l]
        Sb = P.sb("Sb" + sfx, [128, 512], F32)
        Sf = P.sb("Sf" + sfx, [128, 512], F32)
        Sfb = [P.sb("Sfb%d%s" % (i, sfx), [128, 512], BF16) for i in range(2)]
        ld = {}
        for nm in ("q", "qf", "qb", "k", "kdf", "vf", "sbd", "gs"):
            ld[nm] = [P.sb("%s%d%s" % (nm, i, sfx), [128, 4, 512], BF16) for i in range(2)]
        ld["kdb"], ld["vb"] = ld["kdf"], ld["vf"]
        sbst = [P.sb("sbst%d%s" % (i, sfx), [128, 4, 512], BF16) for i in range(2)]
        rst = [P.sb("rst%d%s" % (i, sfx), [128, 4, 512], BF16) for i in range(2)]
        msk = [P.sb("msk%d%s" % (i, sfx), [128, 512], BF16) for i in range(2)]
        of = [P.sb("of%d%s" % (i, sfx), [128, 512], F32) for i in range(2)]
        obf = [P.sb("obf%d%s" % (i, sfx), [128, 512], BF16) for i in range(2)]
        osq = [P.sb("osq%d%s" % (i, sfx), [128, 512], BF16) for i in range(2)]
        m2 = P.sb("gm2" + sfx, [128, 512], F32)
        rs = P.sb("grs" + sfx, [128, 512], F32)
        CDb = [CD[:, d, :].unsqueeze(2).to_broadcast([128, 4, 128]) for d in range(2)]
        flag = cpk[:, O_FLAG:O_FLAG + 1]
        GNG = vpl[:, 108:112].unsqueeze(2).to_broadcast([128, 4, 128])
        GNB = vpl[:, 112:116].unsqueeze(2).to_broadcast([128, 4, 128])

        def v3(ap):
            return ap.rearrange("p (a b) -> p a b", b=128)

        def tokrows(ap_dram, tg):
            return ap_dram[tg * TS:(tg + 1) * TS, :].rearrange("(c p) e -> p c e", p=128)

        def fm(ap_dram, tg):
            return ap_dram[:, tg * TS:(tg + 1) * TS].rearrange("(c p) t -> p c t", p=128)

        P.op(P.pool, lambda: nc.gpsimd.memset(Sb[:, :], 0.0), [], [Sb])
        P.op(P.pool, lambda: nc.gpsimd.memset(Sf[:, :], 0.0), [], [Sf])
        cstate = {"k": 0}
        if C.pending:
            ccin = [P.sb("ccin%d%s" % (i, sfx), [128, CW], F32) for i in range(3)]
            ccout = [P.sb("ccout%d%s" % (i, sfx), [128, CW], BF16) for i in range(3)]

        def cast_step():
            if C.pending:
                k_ = cstate["k"]
                C.cast_job(C.pending.pop(0), ccin[k_ % 3], ccout[k_ % 3], ("act", "dve")[k_ % 2])
                cstate["k"] += 1
        ntg = NTL
        for tg in reversed(range(ntg)):
            kdb_t, v_t, st = ld["kdb"][tg % 2], ld["vb"][tg % 2], sbst[tg % 2]
            P.dma(P.sp, kdb_t[:, :, :], tokrows(KDB, tg), [KDB], [kdb_t], kdb_t)
            P.dma(P.sp, v_t[:, :, :], tokrows(VR, tg), [VR], [v_t], v_t)
            for tc in reversed(range(4)):
                n = tg * 4 + tc
                if n == 63:
                    P.ts("dve", Sb[:, :], Sb[:, :], flag, None, ALU.mult, None, [Sb, cpk], [Sb])
                P.cp("act", st[:, tc, :], Sb[:, :], [Sb], [st])
                pk = PS[n % 2]
                for h in range(4):
                    hs = slice(h * 128, (h + 1) * 128)
                    P.mm(pk[:, hs], kdb_t[:, tc, hs], v_t[:, tc, hs], True, True, [kdb_t, v_t], [pk], sig=(h == 3))
                P.tt("dve", v3(Sb[:, :]), v3(Sb[:, :]), CDb[1], ALU.mult, [Sb, CD], [Sb])
                P.tt("dve", Sb[:, :], Sb[:, :], pk[:, :], ALU.add, [Sb, pk], [Sb])
                cast_step()
            P.dma(P.pool, SBD[tg * 4:(tg + 1) * 4, :, :].rearrange("c p w -> p c w"), st[:, :, :], [st], [SBD], st)
        NCHK = ntg * 4
        grp = {}

        def load_group(tg):
            i2 = tg % 2
            g_ = dict(q=ld["q"][i2], qf=ld["qf"][i2], qb=ld["qb"][i2], k=ld["k"][i2], kdf=ld["kdf"][i2],
                      v=ld["vf"][i2], sbd=ld["sbd"][i2], gs=ld["gs"][i2], rt=rst[i2])
            for (tt_, dr) in ((g_["q"], QT), (g_["qf"], QFT), (g_["qb"], QBT), (g_["k"], KT), (g_["gs"], GST)):
                P.dma(P.sp, tt_[:, :, :], fm(dr, tg), [dr], [tt_], tt_)
            P.dma(P.sp, g_["kdf"][:, :, :], tokrows(KDF, tg), [KDF], [g_["kdf"]], g_["kdf"])
            P.dma(P.sp, g_["v"][:, :, :], tokrows(VR, tg), [VR], [g_["v"]], g_["v"])
            P.dma(P.sp, g_["sbd"][:, :, :], SBD[tg * 4:(tg + 1) * 4, :, :].rearrange("c p w -> p c w"), [SBD],
                  [g_["sbd"]], g_["sbd"])
            grp[tg] = g_

        def stage_a(n):
            tg, tc = n // 4, n % 4
            if tc == 0:
                load_group(tg)
            g_ = grp[tg]
            p2 = n % 2
            cs_ = slice(tc * 128, (tc + 1) * 128)
            if n == 64:
                P.ts("dve", Sf[:, :], Sf[:, :], flag, None, ALU.mult, None, [Sf, cpk], [Sf])
            sfb = Sfb[p2]
            P.cp("act", sfb[:, :], Sf[:, :], [Sf], [sfb])
            pss, pso, pk = PS[p2], PS[2 + p2], PS[4 + p2]
            for h in range(4):
                hs = slice(h * 128, (h + 1) * 128)
                P.mm(pss[:, hs], g_["k"][:, h, cs_], g_["q"][:, h, cs_], True, True, [g_["k"], g_["q"]], [pss], sig=(h == 3))
            mk_ = msk[p2]
            P.tt("dve", mk_[:, :], pss[:, :], maskT[:, :], ALU.mult, [pss, maskT], [mk_])
            for h in range(4):
                hs = slice(h * 128, (h + 1) * 128)
                P.mm(pk[:, hs], g_["kdf"][:, tc, hs], g_["v"][:, tc, hs], True, True, [g_["kdf"], g_["v"]], [pk], sig=(h == 3))
            for h in range(4):
                hs = slice(h * 128, (h + 1) * 128)
                P.mm(pso[:, hs], g_["v"][:, tc, hs], mk_[:, hs], True, False, [g_["v"], mk_], [pso], sig=False)
                P.mm(pso[:, hs], sfb[:, hs], g_["qf"][:, h, cs_], False, False, [sfb, g_["qf"]], [pso], sig=False)
                P.mm(pso[:, hs], g_["sbd"][:, tc, hs], g_["qb"][:, h, cs_], False, True, [g_["sbd"], g_["qb"]], [pso], sig=(h == 3))
            P.tt("dve", v3(Sf[:, :]), v3(Sf[:, :]), CDb[0], ALU.mult, [Sf, CD], [Sf])
            P.tt("dve", Sf[:, :], Sf[:, :], pk[:, :], ALU.add, [Sf, pk], [Sf])

        def stage_b(n):
            tg, tc = n // 4, n % 4
            g_ = grp[tg]
            p2 = n % 2
            cs_ = slice(tc * 128, (tc + 1) * 128)
            pso = PS[2 + p2]
            o_f, o_b, o_q = of[p2], obf[p2], osq[p2]
            P.actf(o_b[:, :], pso[:, :], AF.Copy, [pso], [o_b])
            P.actf(o_q[:, :], pso[:, :], AF.Square, [pso], [o_q])
            P.cp("dve", o_f[:, :], pso[:, :], [pso], [o_f])
            pm, pq = PS[6], PS[7]
            P.mm(pm[:, :], ones128[:, :], o_b[:, :], True, True, [ones128, o_b], [pm])
            P.mm(pq[:, :], ones128[:, :], o_q[:, :], True, True, [ones128, o_q], [pq])
            P.actf(m2[:, :], pm[:, :], AF.Square, [pm], [m2])
            P.tt("dve", rs[:, :], pq[:, :], m2[:, :], ALU.subtract, [pq, m2], [rs])
            P.actf(rs[:, :], rs[:, :], AF.Ln, [rs, epst], [rs], bias=epst[:, 1:2])
            P.actf(rs[:, :], rs[:, :], AF.Exp, [rs], [rs], scale=-0.5)
            P.tt("dve", o_f[:, :], o_f[:, :], pm[:, :], ALU.subtract, [o_f, pm], [o_f])
            P.tt("pool", o_f[:, :], o_f[:, :], rs[:, :], ALU.mult, [o_f, rs], [o_f])
            P.tt("pool", v3(o_f[:, :]), v3(o_f[:, :]), GNG, ALU.mult, [o_f, vpl], [o_f])
            P.tt("dve", v3(o_f[:, :]), v3(o_f[:, :]), GNB, ALU.add, [o_f, vpl], [o_f])
            rt = g_["rt"]
            P.tt("pool", rt[:, :, cs_], v3(o_f[:, :]), g_["gs"][:, :, cs_], ALU.mult, [o_f, g_["gs"]], [rt])
            if tc == 3:
                P.dma(P.pool, fm(RT, tg), rt[:, :, :], [rt], [RT], rt)

        stage_a(0)
        for n in range(NCHK):
            if n + 1 < NCHK:
                stage_a(n + 1)
            stage_b(n)
            cast_step()
        while C.pending:
            cast_step()

    def load_t1(T1, g):
        zv = ZC[g].rearrange("(a b) w -> a b w", b=128)
        for qd in range(4):
            P.dma(P.sp, T1[:, qd * 32:(qd + 1) * 32, :], zv[:, qd * 32:(qd + 1) * 32, :], [ZC], [T1], T1)

    def phase2_fnet(l, T1):
        sfx = "_p2f_%d" % l
        A = P.sb("A" + sfx, [128, 2, 128, 128], BF16)
        YT = P.sb("YT" + sfx, [128, TOK], BF16)
        gb = [P.sb("gb%d%s" % (i, sfx), [128, 4096], BF16) for i in range(2)]
        for g in range(4):
            for cp_ in range(64):
                bank = PS[cp_ % 4]
                for half in range(2):
                    c = 2 * cp_ + half
                    o = bank[:, half * 256:(half + 1) * 256]
                    P.mm(o, T1[:, :, c], W1a, True, False, [T1, cbf], [bank], sig=False)
                    P.mm(o, T1[:, :, 128 + c], W1b, False, True, [T1, cbf], [bank], sig=(half == 1))
                P.cp("act" if cp_ % 2 == 0 else "dve", A[:, :, :, 2 * cp_:2 * cp_ + 2],
                     bank[:, :].rearrange("p (c r s) -> p r s c", c=2, r=2), [bank], [A])
            for a0 in range(0, 64, 4):
                gi = a0 // 4
                gt_ = gb[gi % 2]
                P.dma(P.sp, gt_[:, :], gsc[:, gi * 4096:(gi + 1) * 4096], [gsc], [gt_], gt_)
                if gi == 1 and g + 1 < 4:
                    load_t1(T1, g + 1)
                for b in range(2):
                    bank = PS[4 + (gi * 2 + b) % 4]
                    for ai in range(4):
                        a = a0 + ai
                        for q in range(4):
                            slot = a + 64 * (q // 2)
                            ri = q % 2
                            col = ((ai * 2 + b) * 4 + q) * 128
                            P.mm(bank[:, ai * 128:(ai + 1) * 128], A[:, ri, slot, :], gt_[:, col:col + 128],
                                 q == 0, q == 3, [A, gt_], [bank], sig=(q == 3 and ai == 3))
                    p0 = a0 + 64 * b
                    P.cp("act" if b == 0 else "dve",
                         YT[:, :].rearrange("p (m q) -> p q m", q=128)[:, p0:p0 + 4, :],
                         bank[:, :].rearrange("p (a m) -> p a m", m=128), [bank], [YT])
            for qd in range(4):
                sl = slice(qd * 4096, (qd + 1) * 4096)
                P.dma(P.pool, FT[g * 128:(g + 1) * 128, sl], YT[:, sl], [YT], [FT], YT)

    def phase3(l):
        sfx = "_p3_%d" % l
        W, xt, xb, hb, z, x1b, lntmp = common_tiles(sfx)
        xas = [P.subs(x_, 8) for x_ in xt]
        xbs, zs, x1bs = P.subs(xb, 8), P.subs(z, 8), P.subs(x1b, 8)
        mix = P.sb("mix" + sfx, [128, 8, 512], F32)
        mixs = P.subs(mix, 8)
        gl = [P.sb("gl%d%s" % (i, sfx), [128, 8, 512], BF16) for i in range(3)]
        bi = [P.sb("bi%d%s" % (i, sfx), [128, 4, 512], BF16) for i in range(3)]
        tmp = lntmp[3]
        dst = XA if l < DEPTH - 1 else yT
        vpl = vp[l]

        def fm(ap_dram, tsl):
            return ap_dram[:, tsl].rearrange("(c p) t -> p c t", p=128)

        def load_x(t):
            tsl_ = slice(t * TS, (t + 1) * TS)
            P.dma(P.sp, xt[t % 2][:, :, :], fm(X1, tsl_), [X1], xas[t % 2], xt[t % 2])

        def load_br(t):
            tsl_ = slice(t * TS, (t + 1) * TS)
            for br, dr in enumerate((SGT, RT, FT)):
                P.dma(P.sp, bi[br][:, :, :], fm(dr, tsl_), [dr], [bi[br]], bi[br])
                P.dma(P.sp, gl[br][:, :, :], GT[br * 1024:(br + 1) * 1024, tsl_].rearrange("(c p) t -> p c t", p=128),
                      [GT], [gl[br]], gl[br])

        load_br(0)
        load_x(0)
        for t in range(NTL):
            tsl = slice(t * TS, (t + 1) * TS)
            xa = xas[t % 2]
            k_ = 0
            for br, nm in enumerate(("w_br_a", "w_br_b", "w_br_c")):
                for s in range(2):
                    wb, wv = load_w(W, l, nm, s)
                    for j in range(4):
                        c = 4 * s + j
                        bank = PS[k_ % 4]
                        k_ += 1
                        for kc in range(4):
                            P.mm(bank[:, :], wv[:, kc, j * 128:(j + 1) * 128], bi[br][:, kc, :], kc == 0, kc == 3,
                                 [wb, bi[br]], [bank])
                        if br == 0:
                            P.tt("dve", mixs[c].ap(), bank[:, :], gl[br][:, c, :], ALU.mult, [bank, gl[br]], [mixs[c]])
                        else:
                            tp = tmp[c % 2]
                            P.tt("dve", tp[:, :], bank[:, :], gl[br][:, c, :], ALU.mult, [bank, gl[br]], [tp])
                            P.tt("dve" if (c + br) % 3 == 0 else "pool", mixs[c].ap(), mixs[c].ap(), tp[:, :], ALU.add,
                                 [mixs[c], tp], [mixs[c]])
                        if br == 2:
                            P.actf(xbs[c].ap(), mixs[c].ap(), AF.Copy, [mixs[c]], [xbs[c]])
            prev = None
            for s in range(4):
                wb, wv = load_w(W, l, "w_o", s)
                for j in range(2):
                    c = 2 * s + j
                    bank = PS[4 + c % 2]
                    for kc in range(8):
                        P.mm(bank[:, :], wv[:, kc, j * 128:(j + 1) * 128], xb[:, kc, :], kc == 0, kc == 7, [wb, xbs[kc]], [bank])
                    if prev is not None:
                        ln_stats_mm(c - 1, *prev)
                    P.stt("dve", zs[c].ap(), bank[:, :], 1.0 / ALPHA, xa[c].ap(), ALU.mult, ALU.add, [bank, xa[c]], [zs[c]])
                    prev = ln_pre(c, zs, lntmp)
            ln_stats_mm(7, *prev)
            if t + 1 < NTL:
                load_br(t + 1)
                load_x(t + 1)
            ln_post(zs, x1bs, lntmp, vpl, 16, 24)
            ffn_ln(l, 2, W, zs, x1bs, hb, xa, xbs, lntmp, 32, 40, 0.5 / ALPHA)
            P.dma(P.pool, fm(dst, tsl), xt[t % 2][:, :, :], xa, [dst], xt[t % 2])

    for l in range(NLAY):
        layer_setup(l)
        with contextlib.ExitStack() as e1:
            P.es = e1
            phase1(l)
            P.barrier()
        P.new_phase_sems()
        if phases >= 2:
            with contextlib.ExitStack() as e2o:
                P.es = e2o
                if l == 0:
                    C.pending = C.make_jobs(list(range(1, NLAY)), True)
                pre_t1 = not C.pending
                if pre_t1:
                    T1 = P.sb("T1_p2_%d" % l, [128, 128, 256], BF16)
                    load_t1(T1, 0)
                with contextlib.ExitStack() as e2:
                    P.es = e2
                    phase2_ret(l)
                    P.barrier()
                P.new_phase_sems()
                P.es = e2o
                if not pre_t1:
                    T1 = P.sb("T1_p2_%d" % l, [128, 128, 256], BF16)
                    load_t1(T1, 0)
                phase2_fnet(l, T1)
                P.barrier()
            P.new_phase_sems()
        if phases >= 3:
            with contextlib.ExitStack() as e3:
                P.es = e3
                phase3(l)
                P.barrier()
            P.new_phase_sems()
    P.es = es
    P.barrier()
    es.close()
    return nc


def _win_ext_perm():
    cols = list(range(0, 3072))
    cols += list(range(3072, 3584))
    cols += list(range(3584, 4096))
    q0, k0, v0, g0, zc0 = 4096, 4608, 5120, 5632, 6144
    for base in (q0, k0):
        for h in range(4):
            hb_ = base + h * 128
            cols += list(range(hb_, hb_ + 128))
            cols += list(range(hb_ + 64, hb_ + 128)) + list(range(hb_, hb_ + 64))
    cols += list(range(v0, v0 + 512))
    cols += list(range(g0, g0 + 512))
    cols += list(range(zc0, zc0 + 512))
    return np.asarray(cols)


def _col128(v):
    v = np.asarray(v, np.float32)
    return np.ascontiguousarray(v.reshape(-1, 128).T)


def _bc(v):
    return np.broadcast_to(np.asarray(v, np.float32).reshape(1, -1), (128, np.asarray(v).size))


_CONST_CACHE = {}


def _const_tables(kind):
    if kind in _CONST_CACHE:
        return _CONST_CACHE[kind]
    f64 = np.float64
    L = TOK if kind == "prompt" else TOK // 2
    N1 = L // 128
    cpk = np.zeros((128, NCPK), f64)
    cpk[:, 0:128] = np.eye(128)
    c = np.arange(128)
    ang = 2 * np.pi * np.outer(c, c) / 128.0
    cpk[:, 128:256] = np.cos(ang) / np.sqrt(128.0)
    cpk[:, 256:384] = -np.sin(ang) / np.sqrt(128.0)
    W1 = np.zeros((128, 128), np.complex128)
    if kind == "prompt":
        W1 = np.exp(-2j * np.pi * np.outer(c, c) / 128.0)
    else:
        c64 = np.arange(64)
        blk = np.exp(-2j * np.pi * np.outer(c64, c64) / 64.0)
        W1[0:64, 0:64] = blk
        W1[64:128, 64:128] = blk
    cpk[:, 384:512] = W1.real
    cpk[:, 512:640] = W1.imag
    cpk[:, 640:768] = -W1.imag
    cpk[:, 768:896] = W1.real
    j = np.arange(128)[:, None]
    i = np.arange(128)[None, :]
    cpk[:, 896:1024] = np.maximum(i - j, 0)
    cpk[:, 1024:1152] = (i >= j)
    cpk[:, 1152:1280] = np.maximum(j - i, 0)
    cpk[:, 1280:1408] = (j > i)
    cpk[:, 1408:1536] = (i + 1) * np.ones((128, 1))
    cpk[:, 1536:1664] = (128 - i) * np.ones((128, 1))
    cpk[:, 1664] = 127 - np.arange(128)
    cpk[:, 1665] = np.arange(128)
    cpk[:, 1666] = 1.0 if kind == "prompt" else 0.0
    pos = np.arange(TOK) % L
    inv = 10000.0 ** (-np.arange(64, dtype=np.float32) / 64.0).astype(np.float32)
    angr = pos.astype(np.float32)[None, :] * inv[:, None]
    cosr = np.cos(angr.astype(np.float32)).astype(np.float32)
    sinr = np.sin(angr.astype(np.float32)).astype(np.float32)
    cs = np.zeros((128, 2, TOK), np.float32)
    cs[0:64, 0] = cosr
    cs[64:128, 0] = cosr
    cs[0:64, 1] = -sinr
    cs[64:128, 1] = sinr
    n2 = np.arange(128)[:, None]
    m = np.arange(128)[None, :]
    G = np.zeros((128, 64, 2, 4, 128), f64)
    for a in range(64):
        for b in range(2):
            for sidx in range(2):
                if kind == "prompt":
                    if sidx != b:
                        continue
                    s = a + 64 * b
                    th = 2 * np.pi * n2 * (s + 128 * m) / float(L)
                    val = np.ones((128, 128), bool)
                else:
                    k2 = 2 * (m - 64 * sidx) + b
                    val = (m >= 64 * sidx) & (m < 64 * sidx + 64)
                    val = np.broadcast_to(val, (128, 128))
                    th = 2 * np.pi * n2 * (a + 64 * k2) / float(L)
                G[:, a, b, sidx * 2 + 0, :] = np.where(val, np.cos(th), 0.0) / np.sqrt(float(L))
                G[:, a, b, sidx * 2 + 1, :] = np.where(val, np.sin(th), 0.0) / np.sqrt(float(L))
    out = (cpk.astype(np.float32), cs, np.ascontiguousarray(G.reshape(128, GCOLS).astype(np.float32)))
    _CONST_CACHE[kind] = out
    return out


def make_core_inputs(inp, core):
    perm = _win_ext_perm()
    m = {}
    if core in (0, 1):
        x = inp["x_prompt"][core]
        kind = "prompt"
    elif core in (4, 5):
        x = inp["x_sample"][2 * (core - 4):2 * (core - 4) + 2].reshape(TOK, D)
        kind = "sample"
    else:
        x = None
        kind = "prompt"
    zero = x is None
    cpk, cs, G = _const_tables(kind)
    m["cpk"] = cpk
    m["cossin"] = cs
    m["gtab"] = G
    m["xT"] = np.zeros((D, TOK), np.float32) if zero else np.ascontiguousarray(x.T)
    for l in range(DEPTH):
        for (nm, K, N, sw) in WSPECS:
            if zero:
                m["%s_%d" % (nm, l)] = np.zeros((K, N), np.float32)
            elif nm == "w_in":
                m["%s_%d" % (nm, l)] = np.ascontiguousarray(inp["w_in"][l][:, perm])
            else:
                m["%s_%d" % (nm, l)] = np.ascontiguousarray(inp[nm][l])
        b_in = inp["b_in"][l]
        bext = b_in[perm]
        vp = np.concatenate([
            _col128(inp["ln1_g"][l]), _col128(inp["ln1_b"][l]),
            _col128(inp["ln2_g"][l]), _col128(inp["ln2_b"][l]),
            _col128(inp["ln3_g"][l]), _col128(inp["ln3_b"][l]),
            _col128(bext), _col128(inp["ret_gn_g"][l]), _col128(inp["ret_gn_b"][l])], axis=1)
        m["vecp_%d" % l] = np.zeros_like(vp) if zero else np.ascontiguousarray(vp)
        wsT = np.concatenate([inp["sgu_w"][l][g].T for g in range(4)], axis=1)
        lp = np.concatenate([
            wsT, _bc(b_in[3584:4096]), _bc(inp["sgu_ln_g"][l]), _bc(inp["sgu_ln_b"][l]),
            _bc(b_in[5120:5632]), _bc(inp["sgu_b"][l].reshape(-1)),
            _bc(inp["ret_logit_fwd"][l]), _bc(inp["ret_logit_bwd"][l])], axis=1).astype(np.float32)
        assert lp.shape == (128, NLP)
        m["lp_%d" % l] = np.zeros_like(lp) if zero else np.ascontiguousarray(lp)
    return m


_NC_CACHE = {}


def kernel(**inputs):
    inp = {k: np.asarray(v) for k, v in inputs.items()}
    if "nc" not in _NC_CACHE:
        _NC_CACHE["nc"] = build_program()
    nc = _NC_CACHE["nc"]
    in_maps = [make_core_inputs(inp, c) for c in range(8)]
    res = run_bass_kernel_spmd(nc, in_maps, core_ids=list(range(8)))
    outs = [res.results[c]["yT"] for c in range(8)]
    y_prompt = np.stack([np.ascontiguousarray(outs[c].T) for c in (0, 1)], axis=0)
    y_sample = np.concatenate([outs[c].T.reshape(2, TOK // 2, D) for c in (4, 5)], axis=0)
    return (y_prompt.astype(np.float32), np.ascontiguousarray(y_sample).astype(np.float32))
```
